# Optimizing a Trainium2 kernel written in Bass

```python
import math
import jax
import jax.numpy as jnp
from jax import lax
import numpy as np

D_MODEL = 4096
BATCH = 2
SEQ = 4096
DEPTH = 4

N_EVEN = (DEPTH + 1) // 2
N_ODD = DEPTH // 2
EPS = 1e-6
BLOCK = 128
HEAD_DIM = 128
A_HEADS = 16
A_KV_HEADS = 4
A_GROUP = A_HEADS // A_KV_HEADS
WINDOW = 128
ROPE_THETA = 500000.0
ROPE_DIM = HEAD_DIM // 4
S5_WIDTH = 1024
S5_GROUP = 16
S5_GROUPS = S5_WIDTH // S5_GROUP
S5_STATE = 64
C_HEADS = 16
LRU_WIDTH = 2048
LRU_BLOCKS = 16
LRU_BLOCK_DIM = LRU_WIDTH // LRU_BLOCKS
LRU_CONV = 4
LRU_C = 8.0
D_FF = 10240
FFN_CONV = 3
PLE_DIM = 256

A_Q = A_HEADS * HEAD_DIM
A_KV = A_KV_HEADS * HEAD_DIM
EVEN_IN = A_Q + 2 * A_KV + S5_WIDTH
EVEN_OUT = A_Q + S5_WIDTH
C_W = C_HEADS * HEAD_DIM
ODD_IN = 3 * C_W + 2 * LRU_WIDTH
ODD_OUT = C_W + LRU_WIDTH

kernel_name = 'hybrid_swa_s5_stickbreak_rglru_trunk'


def rmsnorm(x, g):
    xf = x.astype(jnp.float32)
    y = xf * lax.rsqrt(jnp.mean(xf * xf, axis=-1, keepdims=True) + EPS)
    return (y * g.astype(jnp.float32)).astype(x.dtype)


def causal_depthwise_conv(x, w, b):
    width, seq = w.shape[0], x.shape[1]
    xp = jnp.pad(x, ((0, 0), (width - 1, 0), (0, 0)))
    y = b
    for tap in range(width):
        y = y + xp[:, tap:tap + seq] * w[tap]
    return y


def partial_rope(x, positions):
    half = ROPE_DIM // 2
    inv = ROPE_THETA ** (-jnp.arange(half, dtype=jnp.float32) * (2.0 / ROPE_DIM))
    ang = positions.astype(jnp.float32)[:, :, None] * inv
    cos = jnp.cos(ang)[:, :, None, :]
    sin = jnp.sin(ang)[:, :, None, :]
    xr = x[..., :ROPE_DIM].astype(jnp.float32)
    x1, x2 = xr[..., :half], xr[..., half:]
    rot = jnp.concatenate([x1 * cos - x2 * sin, x2 * cos + x1 * sin], axis=-1).astype(x.dtype)
    return jnp.concatenate([rot, x[..., ROPE_DIM:]], axis=-1)


def sliding_window_attention(q, k, v, sinks):
    bsz, seq = q.shape[0], q.shape[1]
    nb = seq // BLOCK
    qb = q.reshape(bsz, nb, BLOCK, A_KV_HEADS, A_GROUP, HEAD_DIM)

    def banded(t):
        cur = t.reshape(bsz, nb, BLOCK, A_KV_HEADS, HEAD_DIM)
        prev = jnp.pad(cur, ((0, 0), (1, 0), (0, 0), (0, 0), (0, 0)))[:, :-1]
        return jnp.concatenate([prev, cur], axis=2)

    kb, vb = banded(k), banded(v)
    logits = jnp.einsum('bnqkgd,bnskd->bnkgqs', qb, kb,
                        preferred_element_type=jnp.float32) * (HEAD_DIM ** -0.5)
    qpos = jnp.arange(BLOCK)[:, None] + BLOCK
    kpos = jnp.arange(2 * BLOCK)[None, :]
    blk = jnp.arange(nb)[:, None, None]
    valid = (kpos <= qpos) & (qpos - kpos < WINDOW) & (blk * BLOCK - BLOCK + kpos >= 0)
    logits = jnp.where(valid[None, :, None, None, :, :], logits, -jnp.inf)
    sink = sinks.astype(jnp.float32).reshape(A_KV_HEADS, A_GROUP)[None, None, :, :, None, None]
    m = jnp.maximum(jnp.max(logits, axis=-1, keepdims=True), sink)
    w = jnp.exp(logits - m)
    probs = (w / (jnp.sum(w, axis=-1, keepdims=True) + jnp.exp(sink - m))).astype(v.dtype)
    out = jnp.einsum('bnkgqs,bnskd->bnqkgd', probs, vb)
    return out.reshape(bsz, seq, A_Q)


def s5_layer(u, lam_re, lam_im, log_dt, b_re, b_im, c_re, c_im, d_skip, glu_w, glu_b):
    bsz, seq = u.shape[0], u.shape[1]
    uf = u.astype(jnp.float32).reshape(bsz, seq, S5_GROUPS, S5_GROUP)
    dt = jnp.exp(log_dt.astype(jnp.float32))[:, None]
    lr = jnp.minimum(lam_re.astype(jnp.float32), -1e-4)
    li = lam_im.astype(jnp.float32)
    mag = jnp.exp(lr * dt)
    ab_re, ab_im = mag * jnp.cos(li * dt), mag * jnp.sin(li * dt)
    nr, ni = ab_re - 1.0, ab_im
    den = lr * lr + li * li
    f_re, f_im = (nr * lr + ni * li) / den, (ni * lr - nr * li) / den
    br, bi = b_re.astype(jnp.float32), b_im.astype(jnp.float32)
    bb_re = f_re[..., None] * br - f_im[..., None] * bi
    bb_im = f_re[..., None] * bi + f_im[..., None] * br
    xu_re = jnp.einsum('bsgp,gnp->bsgn', uf, bb_re)
    xu_im = jnp.einsum('bsgp,gnp->bsgn', uf, bb_im)
    a_re = jnp.broadcast_to(ab_re, xu_re.shape)
    a_im = jnp.broadcast_to(ab_im, xu_im.shape)

    def combine(left, right):
        ar1, ai1, hr1, hi1 = left
        ar2, ai2, hr2, hi2 = right
        return (ar2 * ar1 - ai2 * ai1, ar2 * ai1 + ai2 * ar1,
                ar2 * hr1 - ai2 * hi1 + hr2, ar2 * hi1 + ai2 * hr1 + hi2)

    _, _, h_re, h_im = lax.associative_scan(combine, (a_re, a_im, xu_re, xu_im), axis=1)
    y = (jnp.einsum('bsgn,gpn->bsgp', h_re, c_re.astype(jnp.float32))
         - jnp.einsum('bsgn,gpn->bsgp', h_im, c_im.astype(jnp.float32))
         + d_skip.astype(jnp.float32).reshape(S5_GROUPS, S5_GROUP) * uf)
    g = jax.nn.gelu(y.reshape(bsz, seq, S5_WIDTH))
    out = g * jax.nn.sigmoid(g @ glu_w.astype(jnp.float32) + glu_b.astype(jnp.float32))
    return out.astype(u.dtype)


def stick_breaking_attention(q, k, v):
    bsz, seq = q.shape[0], q.shape[1]
    scale = HEAD_DIM ** -0.5
    outs = []
    for i in range(seq // BLOCK):
        q0, L = i * BLOCK, (i + 1) * BLOCK
        z = jnp.einsum('bqhd,bshd->bhqs', q[:, q0:L], k[:, :L],
                       preferred_element_type=jnp.float32) * scale
        strict = jnp.arange(L)[None, :] < (q0 + jnp.arange(BLOCK))[:, None]
        log_beta = jax.nn.log_sigmoid(z)
        log_stay = jnp.where(strict, jax.nn.log_sigmoid(-z), 0.0)
        between = lax.cumsum(log_stay, axis=3, reverse=True) - log_stay
        w = jnp.where(strict, jnp.exp(log_beta + between), 0.0).astype(v.dtype)
        outs.append(jnp.einsum('bhqs,bshd->bqhd', w, v[:, :L]))
    return jnp.concatenate(outs, axis=1).reshape(bsz, seq, C_W)


def rglru_block(xr, xg, conv_w, conv_b, wa, ba, wx, bx, lam):
    bsz, seq = xr.shape[0], xr.shape[1]
    xc = causal_depthwise_conv(xr, conv_w, conv_b)
    xb = xc.reshape(bsz, seq, LRU_BLOCKS, LRU_BLOCK_DIM)
    r = jax.nn.sigmoid(jnp.einsum('bshi,hij->bshj', xb, wa) + ba).reshape(bsz, seq, LRU_WIDTH)
    ig = jax.nn.sigmoid(jnp.einsum('bshi,hij->bshj', xb, wx) + bx).reshape(bsz, seq, LRU_WIDTH)
    log_a = -LRU_C * r.astype(jnp.float32) * jax.nn.softplus(-lam.astype(jnp.float32))
    a = jnp.exp(log_a)
    drive = jnp.sqrt(-jnp.expm1(2.0 * log_a)) * (ig * xc).astype(jnp.float32)

    def combine(left, right):
        a1, h1 = left
        a2, h2 = right
        return a2 * a1, a2 * h1 + h2

    _, h = lax.associative_scan(combine, (a, drive), axis=1)
    return h.astype(xr.dtype) * jax.nn.gelu(xg)


def conv_ffn(n, w_up, conv_w, conv_b, w_down):
    h = causal_depthwise_conv(n @ w_up, conv_w, conv_b)
    g, u = jnp.split(h, 2, axis=-1)
    return (jax.nn.silu(g) * u) @ w_down


def _normal(key, shape, scale):
    return jax.random.normal(key, shape, jnp.float32) * scale


def setup_inputs(seed: int = 0) -> dict:
    key = jax.random.key(seed)
    ks = iter(jax.random.split(key, 48))
    f32 = jnp.float32
    x = _normal(next(ks), (BATCH, SEQ, D_MODEL), 1.0)
    p = _normal(next(ks), (DEPTH, BATCH, SEQ, PLE_DIM), 1.0)
    positions = (jax.random.randint(next(ks), (BATCH, 1), 0, 1024, jnp.int32)
                 + jnp.arange(SEQ, dtype=jnp.int32)[None, :])
    mix_norm = 1.0 + _normal(next(ks), (DEPTH, D_MODEL), 0.02)
    ffn_norm = 1.0 + _normal(next(ks), (DEPTH, D_MODEL), 0.02)
    ple_norm = 1.0 + _normal(next(ks), (DEPTH, D_MODEL), 0.02)
    final_norm = 1.0 + _normal(next(ks), (D_MODEL,), 0.02)
    ab_w_in = _normal(next(ks), (N_EVEN, D_MODEL, EVEN_IN), D_MODEL ** -0.5)
    ab_w_out = _normal(next(ks), (N_EVEN, EVEN_OUT, D_MODEL), EVEN_OUT ** -0.5)
    attn_sinks = _normal(next(ks), (N_EVEN, A_HEADS), 0.5)
    s5_lam_re = -0.5 + _normal(next(ks), (N_EVEN, S5_GROUPS, S5_STATE), 0.01)
    s5_lam_im = (math.pi * jnp.arange(S5_STATE, dtype=f32)
                 + _normal(next(ks), (N_EVEN, S5_GROUPS, S5_STATE), 0.01))
    s5_log_dt = jax.random.uniform(next(ks), (N_EVEN, S5_GROUPS), f32, math.log(1e-3), math.log(1e-1))
    s5_b_re = _normal(next(ks), (N_EVEN, S5_GROUPS, S5_STATE, S5_GROUP), (2 * S5_GROUP) ** -0.5)
    s5_b_im = _normal(next(ks), (N_EVEN, S5_GROUPS, S5_STATE, S5_GROUP), (2 * S5_GROUP) ** -0.5)
    s5_c_re = _normal(next(ks), (N_EVEN, S5_GROUPS, S5_GROUP, S5_STATE), S5_STATE ** -0.5)
    s5_c_im = _normal(next(ks), (N_EVEN, S5_GROUPS, S5_GROUP, S5_STATE), S5_STATE ** -0.5)
    s5_d = _normal(next(ks), (N_EVEN, S5_WIDTH), 1.0)
    s5_glu_w = _normal(next(ks), (N_EVEN, S5_WIDTH, S5_WIDTH), S5_WIDTH ** -0.5)
    s5_glu_b = _normal(next(ks), (N_EVEN, S5_WIDTH), 0.01)
    cd_w_in = _normal(next(ks), (N_ODD, D_MODEL, ODD_IN), D_MODEL ** -0.5)
    cd_w_out = _normal(next(ks), (N_ODD, ODD_OUT, D_MODEL), ODD_OUT ** -0.5)
    lru_conv_w = _normal(next(ks), (N_ODD, LRU_CONV, LRU_WIDTH), LRU_CONV ** -0.5)
    lru_conv_b = _normal(next(ks), (N_ODD, LRU_WIDTH), 0.01)
    lru_wa = _normal(next(ks), (N_ODD, LRU_BLOCKS, LRU_BLOCK_DIM, LRU_BLOCK_DIM), LRU_BLOCK_DIM ** -0.5)
    lru_ba = _normal(next(ks), (N_ODD, LRU_BLOCKS, LRU_BLOCK_DIM), 0.01)
    lru_wx = _normal(next(ks), (N_ODD, LRU_BLOCKS, LRU_BLOCK_DIM, LRU_BLOCK_DIM), LRU_BLOCK_DIM ** -0.5)
    lru_bx = _normal(next(ks), (N_ODD, LRU_BLOCKS, LRU_BLOCK_DIM), 0.01)
    a_pow_c = jax.random.uniform(next(ks), (N_ODD, LRU_WIDTH), f32, 0.9, 0.999)
    s = a_pow_c ** (1.0 / LRU_C)
    lru_lambda = jnp.log(s) - jnp.log1p(-s)
    ffn_w_up = _normal(next(ks), (DEPTH, D_MODEL, 2 * D_FF), D_MODEL ** -0.5)
    ffn_conv_w = _normal(next(ks), (DEPTH, FFN_CONV, 2 * D_FF), FFN_CONV ** -0.5)
    ffn_conv_b = _normal(next(ks), (DEPTH, 2 * D_FF), 0.01)
    ffn_w_down = _normal(next(ks), (DEPTH, D_FF, D_MODEL), D_FF ** -0.5)
    ple_w_gate = _normal(next(ks), (DEPTH, D_MODEL, D_MODEL), D_MODEL ** -0.5)
    ple_w_proj = _normal(next(ks), (DEPTH, PLE_DIM, D_MODEL), PLE_DIM ** -0.5)
    return {'x': x, 'p': p, 'positions': positions,
            'mix_norm': mix_norm, 'ffn_norm': ffn_norm, 'ple_norm': ple_norm, 'final_norm': final_norm,
            'ab_w_in': ab_w_in, 'ab_w_out': ab_w_out, 'attn_sinks': attn_sinks,
            's5_lam_re': s5_lam_re, 's5_lam_im': s5_lam_im, 's5_log_dt': s5_log_dt,
            's5_b_re': s5_b_re, 's5_b_im': s5_b_im, 's5_c_re': s5_c_re, 's5_c_im': s5_c_im,
            's5_d': s5_d, 's5_glu_w': s5_glu_w, 's5_glu_b': s5_glu_b,
            'cd_w_in': cd_w_in, 'cd_w_out': cd_w_out, 'lru_conv_w': lru_conv_w, 'lru_conv_b': lru_conv_b,
            'lru_wa': lru_wa, 'lru_ba': lru_ba, 'lru_wx': lru_wx, 'lru_bx': lru_bx, 'lru_lambda': lru_lambda,
            'ffn_w_up': ffn_w_up, 'ffn_conv_w': ffn_conv_w, 'ffn_conv_b': ffn_conv_b, 'ffn_w_down': ffn_w_down,
            'ple_w_gate': ple_w_gate, 'ple_w_proj': ple_w_proj}


def reference(x, p, positions, mix_norm, ffn_norm, ple_norm, final_norm,
              ab_w_in, ab_w_out, attn_sinks,
              s5_lam_re, s5_lam_im, s5_log_dt, s5_b_re, s5_b_im, s5_c_re, s5_c_im,
              s5_d, s5_glu_w, s5_glu_b,
              cd_w_in, cd_w_out, lru_conv_w, lru_conv_b, lru_wa, lru_ba, lru_wx, lru_bx, lru_lambda,
              ffn_w_up, ffn_conv_w, ffn_conv_b, ffn_w_down, ple_w_gate, ple_w_proj):
    bsz, seq = x.shape[0], x.shape[1]
    h = x
    for i in range(DEPTH):
        j = i // 2
        n = rmsnorm(h, mix_norm[i])
        if i % 2 == 0:
            q, k, v, u = jnp.split(n @ ab_w_in[j], [A_Q, A_Q + A_KV, A_Q + 2 * A_KV], axis=-1)
            q = partial_rope(q.reshape(bsz, seq, A_HEADS, HEAD_DIM), positions)
            k = partial_rope(k.reshape(bsz, seq, A_KV_HEADS, HEAD_DIM), positions)
            v = v.reshape(bsz, seq, A_KV_HEADS, HEAD_DIM)
            ya = sliding_window_attention(q, k, v, attn_sinks[j])
            yb = s5_layer(u, s5_lam_re[j], s5_lam_im[j], s5_log_dt[j], s5_b_re[j], s5_b_im[j],
                          s5_c_re[j], s5_c_im[j], s5_d[j], s5_glu_w[j], s5_glu_b[j])
            h = h + jnp.concatenate([ya, yb], axis=-1) @ ab_w_out[j]
        else:
            q, k, v, xr, xg = jnp.split(n @ cd_w_in[j],
                                        [C_W, 2 * C_W, 3 * C_W, 3 * C_W + LRU_WIDTH], axis=-1)
            yc = stick_breaking_attention(q.reshape(bsz, seq, C_HEADS, HEAD_DIM),
                                          k.reshape(bsz, seq, C_HEADS, HEAD_DIM),
                                          v.reshape(bsz, seq, C_HEADS, HEAD_DIM))
            yd = rglru_block(xr, xg, lru_conv_w[j], lru_conv_b[j], lru_wa[j], lru_ba[j],
                             lru_wx[j], lru_bx[j], lru_lambda[j])
            h = h + jnp.concatenate([yc, yd], axis=-1) @ cd_w_out[j]
        h = h + conv_ffn(rmsnorm(h, ffn_norm[i]), ffn_w_up[i], ffn_conv_w[i], ffn_conv_b[i], ffn_w_down[i])
        gate = jax.nn.sigmoid(rmsnorm(h, ple_norm[i]) @ ple_w_gate[i])
        h = h + gate * (p[i] @ ple_w_proj[i])
    return rmsnorm(h, final_norm)
```

```python
import math
from contextlib import ExitStack
import numpy as np
import concourse.bass as bass
import concourse.mybir as mybir
from concourse.bass_utils import run_bass_kernel_spmd

EPS = 1e-6

F32 = mybir.dt.float32
BF16 = mybir.dt.bfloat16
I32 = mybir.dt.int32
AF = mybir.ActivationFunctionType
ALU = mybir.AluOpType
AX = mybir.AxisListType

EPOCH = 30000
NDSEM = 24


class Prog:
    def __init__(self, nc, es):
        self.nc, self.es = nc, es
        self.ges = es
        self.eng = dict(pe=nc.tensor, act=nc.scalar, dve=nc.vector, pool=nc.gpsimd, sp=nc.sync)
        self.sems = {}
        self.cnt = {k: 0 for k in self.eng}
        self.epoch = {k: 0 for k in self.eng}
        for k in self.eng:
            self.sems[(k, 0)] = es.enter_context(nc.semaphore("s_%s_0" % k))
        self.seen = {k: {} for k in self.eng}
        self.res = {}
        self.dsem = [es.enter_context(nc.semaphore("d_%d" % i)) for i in range(NDSEM)]
        self.dcnt = [0] * NDSEM
        self.dnext = 0
        self.nbuf = 0
        self.out_marks = []

    def sb(self, shape, dt=F32, name=None):
        self.nbuf += 1
        return self.es.enter_context(self.nc.sbuf_tensor("t%d_%s" % (self.nbuf, name or "b"), list(shape), dt))

    def ps(self, shape, dt=F32, name=None):
        self.nbuf += 1
        return self.es.enter_context(self.nc.psum_tensor("t%d_%s" % (self.nbuf, name or "p"), list(shape), dt))

    def _sem(self, key):
        if key[0] == 'd':
            return self.dsem[key[1]]
        return self.sems[key]

    def _need(self, R, W):
        need = {}

        def add(kv):
            if kv is None:
                return
            k, v = kv
            if need.get(k, 0) < v:
                need[k] = v
        for r in R:
            st = self.res.get(r)
            if st:
                add(st[0])
        for w in W:
            st = self.res.get(w)
            if st:
                add(st[0])
                for x in st[1]:
                    add(x)
        return need

    def _waits(self, e, need, skip_self_pe=True):
        eng = self.eng[e]
        for k, v in need.items():
            if e == 'pe' and k[0] == 'pe':
                continue
            if self.seen[e].get(k, 0) >= v:
                continue
            eng.wait_ge(self._sem(k), v)
            self.seen[e][k] = v

    def _mark(self, R, W, kv):
        for r in R:
            st = self.res.setdefault(r, [None, []])
            st[1].append(kv)
        for w in W:
            self.res[w] = [kv, []]

    def I(self, e, fn, R=(), W=()):
        need = self._need(R, W)
        self._waits(e, need)
        if self.cnt[e] >= EPOCH:
            self.epoch[e] += 1
            self.cnt[e] = 0
            self.sems[(e, self.epoch[e])] = self.ges.enter_context(
                self.nc.semaphore("s_%s_%d" % (e, self.epoch[e])))
        key = (e, self.epoch[e])
        ins = fn(self.eng[e])
        ins.then_inc(self.sems[key], 1)
        self.cnt[e] += 1
        kv = (key, self.cnt[e])
        self._mark(R, W, kv)
        return kv

    def dma(self, q, out, in_, R=(), W=(), **kw):
        need = self._need(R, W)
        j = self.dnext
        self.dnext = (self.dnext + 1) % NDSEM
        if self.dcnt[j] > 0:
            need[('d', j)] = max(need.get(('d', j), 0), self.dcnt[j])
        self._waits(q, need)
        ins = self.eng[q].dma_start(out=out, in_=in_, **kw)
        ins.then_inc(self.dsem[j], 16)
        self.dcnt[j] += 16
        kv = (('d', j), self.dcnt[j])
        self._mark(R, W, kv)
        return kv

    def wait_all(self, e, keys):
        need = self._need(keys, ())
        self._waits(e, need)

    def barrier(self):
        allk = {}
        for k in self.eng:
            if self.cnt[k]:
                allk[(k, self.epoch[k])] = self.cnt[k]
        for j in range(NDSEM):
            if self.dcnt[j]:
                allk[('d', j)] = self.dcnt[j]
        for e in self.eng:
            self._waits(e, dict(allk))

    def mm(self, out, lhsT, rhs, start, stop, R, W):
        return self.I('pe', lambda t: t.matmul(out, lhsT=lhsT, rhs=rhs, start=start, stop=stop), R, W)

    def act(self, out, in_, func, R, W, bias=None, scale=None, accum_out=None, e='act'):
        kw = {}
        if bias is not None:
            kw['bias'] = bias
        if scale is not None:
            kw['scale'] = scale
        if accum_out is not None:
            kw['accum_out'] = accum_out
        return self.I(e, lambda t: t.activation(out=out, in_=in_, func=func, **kw), R, W)

    def tt(self, out, in0, in1, op, R, W, e='dve'):
        return self.I(e, lambda t: t.tensor_tensor(out=out, in0=in0, in1=in1, op=op), R, W)

    def ts(self, out, in0, s1, op0, R, W, s2=None, op1=None, e='dve', accum_out=None):
        kw = {}
        if op1 is not None:
            kw['op1'] = op1
        if accum_out is not None:
            kw['accum_out'] = accum_out
        return self.I(e, lambda t: t.tensor_scalar(out=out, in0=in0, scalar1=s1, scalar2=s2, op0=op0, **kw), R, W)

    def stt(self, out, in0, scalar, in1, op0, op1, R, W, e='dve'):
        return self.I(e, lambda t: t.scalar_tensor_tensor(out=out, in0=in0, scalar=scalar, in1=in1,
                                                           op0=op0, op1=op1), R, W)

    def copy(self, out, in_, R, W, e='dve'):
        if e == 'act':
            return self.I('act', lambda t: t.copy(out=out, in_=in_), R, W)
        return self.I(e, lambda t: t.tensor_copy(out=out, in_=in_), R, W)

    def memset(self, ap, val, W, e='dve'):
        return self.I(e, lambda t: t.memset(ap, val), (), W)

class WStream:
    def __init__(self, p, nslots, kcmax, gcols, pf):
        self.p = p
        self.slots = [p.sb([128, kcmax, gcols], BF16, name="wslot%d" % i) for i in range(nslots)]
        self.plan = []
        self.issued = 0
        self.taken = 0
        self.pf = pf

    def add(self, ap, kc, ncols):
        self.plan.append((ap, kc, ncols))

    def _issue(self, i):
        ap, kc, ncols = self.plan[i]
        s = i % len(self.slots)
        self.p.dma('pool', self.slots[s][:, 0:kc, 0:ncols], ap.rearrange("p (kc m) -> p kc m", kc=kc),
                   W=['wslot%d' % s])

    def get(self):
        i = self.taken
        self.taken += 1
        hi = min(len(self.plan), i + 1 + self.pf)
        hi = min(hi, i + len(self.slots) - 1)
        while self.issued < hi:
            self._issue(self.issued)
            self.issued += 1
        s = i % len(self.slots)
        return self.slots[s], 'wslot%d' % s


def emit_dense(p, cfg, A):
    D, FO, GLU, DFF, FIN, NT, T, PD, NS = (cfg[k] for k in ('D', 'FO', 'GLU', 'DFF', 'FIN', 'NT', 'T', 'PD', 'NS'))
    front = FO > 0
    DC, FOC, FC2, PC = D // 128, FO // 128, DFF // 128, PD // 128
    G = 256
    ntiles = NT // T
    hT_in, cst, tail_g = A['hT_in'], A['ones'], A['tail_g']
    if front:
        yT, pT, w_out = A['yT'], A['pT'], A['w_out']
        if GLU:
            glu_w, glu_b = A['glu_w'], A['glu_b']
        ffn_g, w_up, conv_w, conv_b, w_down = A['ffn_g'], A['w_up'], A['conv_w'], A['conv_b'], A['w_down']
        ple_g, w_gate, w_proj, hT_out = A['ple_g'], A['w_gate'], A['w_proj'], A['hT_out']
    if FIN:
        w_in, projT = A['w_in'], A['projT']
    else:
        outT = A['outT']
    if True:
        TH = T + 2
        xb = p.sb([128, max(DC, FOC), T], BF16, name="xb")
        ones_f = p.sb([128, 128], F32, name="ones_f")
        ones_b = p.sb([128, 128], BF16, name="ones_b")
        tailg_s = p.sb([128, DC], F32, name="tailg")
        rstd = p.sb([128, TH], F32, name="rstd")
        NHC = 4
        hc = [p.sb([128, TH], F32, name="hc%d" % i) for i in range(NHC)]
        ho = [p.sb([128, TH], F32, name="ho%d" % i) for i in range(NHC)]
        sq = [p.sb([128, TH], BF16, name="sq%d" % i) for i in range(2)]
        NPS = 6
        psb = [p.ps([128, 512], F32, name="psb%d" % i) for i in range(NPS)]
        ps_stat = p.ps([128, 512], F32, name="ps_stat")
        ps_halo = None
        ws = WStream(p, 4, 32, G, pf=2)
        cnt = dict(ps=0, hc=0, ho=0, sq=0, E=0, tg=0, hal=0)
        outkeys = []

        p.dma('sp', ones_f[:], cst, W=['ones_f'])
        p.copy(ones_b[:], ones_f[:], R=['ones_f'], W=['ones_b'])
        p.dma('sp', tailg_s[:], tail_g, W=['tailg'])
        if front:
            ffng_s = p.sb([128, DC], F32, name="ffng")
            pleg_s = p.sb([128, DC], F32, name="pleg")
            cw_s = p.sb([128, 3, 2 * FC2], F32, name="cw")
            cb_s = p.sb([128, 2 * FC2], F32, name="cb")
            p.dma('sp', ffng_s[:], ffn_g, W=['ffng'])
            p.dma('sp', pleg_s[:], ple_g, W=['pleg'])
            p.dma('sp', cw_s[:], conv_w, W=['cw'])
            p.dma('sp', cb_s[:], conv_b, W=['cb'])
            if GLU:
                glub_s = p.sb([128, 8], F32, name="glub")
                p.dma('sp', glub_s[:], glu_b, W=['glub'])
                ybt = p.sb([128, 8, T], BF16, name="ybt")
                ybth = None
            FCS = FC2 // NS
            aT = p.sb([128, FCS, T], BF16, name="aT")
            halo = p.sb([128, 2 * FC2, 2], F32, name="halo")
            p.memset(halo[:], 0.0, W=['halo%d' % c for c in range(2 * FC2)])
            xbh = None
            E = [p.sb([128, TH], F32, name="E%d" % i) for i in range(4)]
            cg = [p.sb([128, T], F32, name="cg%d" % i) for i in range(2)]
            cu = [p.sb([128, T], F32, name="cu%d" % i) for i in range(2)]
            sg = [p.sb([128, T], F32, name="sg%d" % i) for i in range(2)]
            pb = p.sb([128, PC, T], BF16, name="pb")
            wpb = [p.sb([128, PC, G], BF16, name="wpb%d" % i) for i in range(2)]
            gt = [p.sb([128, T], F32, name="gt%d" % i) for i in range(2)]

        for ti in range(ntiles):
            if front:
                if GLU:
                    for g in range(1024 // G):
                        ws.add(glu_w[g], 8, G)
                for g in range(D // G):
                    ws.add(w_out[g], FOC, G)
                for s in range(NS):
                    for g in range(FCS // 2):
                        c0 = (s * FCS + 2 * g) * 128
                        ws.add(w_up[c0 // G], DC, G)
                        ws.add(w_up[(DFF + c0) // G], DC, G)
                    for g in range(D // G):
                        ws.add(w_down[s * (D // G) + g], FCS, G)
                for g in range(D // G):
                    ws.add(w_gate[g], DC, G)
            if FIN:
                for g in range(FIN // G):
                    ws.add(w_in[g], DC, G)

        def nxt(kind, n):
            i = cnt[kind] % n
            cnt[kind] += 1
            return i

        def mm_group(wslot, wkey, mc, KC, rhs_fn, rhs_keys, segs):
            outs = []
            for si, (c0, n) in enumerate(segs):
                if si == 0:
                    i = nxt('ps', NPS)
                    pt, key = psb[i][:, 0:n], 'psb%d' % i
                else:
                    j = nxt('hal', 256)
                    pt, key = ps_halo[:, 2 * j:2 * j + n], 'ps_halo'
                for kc in range(KC):
                    p.mm(pt, wslot[:, kc, mc * 128:(mc + 1) * 128], rhs_fn(kc, si),
                         start=(kc == 0), stop=(kc == KC - 1), R=[wkey] + rhs_keys, W=[key])
                outs.append((pt, key))
            return outs

        def load_h(src, m, t0, segs, halo_src):
            i = nxt('hc', NHC)
            p.dma('sp', hc[i][:, 0:T], src[m * 128:(m + 1) * 128, t0:t0 + T], R=['hdram%d' % m], W=['hc%d' % i])
            if len(segs) > 1:
                if halo_src is None:
                    p.copy(hc[i][:, T:TH], h1halo[:, m, :], R=['h1halo'], W=['hc%d_h' % i], e='act')
                else:
                    p.dma('sp', hc[i][:, T:TH], halo_src[m * 128:(m + 1) * 128, :], W=['hc%d_h' % i])
            return i

        def emit_chunk(m, i_o, t0, segs, dst, last_stats):
            p.dma('sp', dst[m * 128:(m + 1) * 128, t0:t0 + T], ho[i_o][:, 0:T], R=['ho%d' % i_o], W=['hdram%d' % m])
            if last_stats:
                j = nxt('sq', 2)
                w = T + (2 if len(segs) > 1 else 0)
                p.act(sq[j][:, 0:w], ho[i_o][:, 0:w], AF.Square, R=['ho%d' % i_o, 'ho%d_h' % i_o], W=['sq%d' % j])
                p.mm(ps_stat[:, 0:T], ones_b[:], sq[j][:, 0:T], start=(m == 0), stop=(m == DC - 1),
                     R=['ones_b', 'sq%d' % j], W=['ps_stat'])
                if len(segs) > 1:
                    p.mm(ps_stat_h[:, 0:2], ones_b[:], sq[j][:, T:TH], start=(m == 0), stop=(m == DC - 1),
                         R=['ones_b', 'sq%d' % j], W=['ps_stat_h'])

        def stats_from(src, t0):
            for m in range(DC):
                i = load_h(src, m, t0, [(0, T)], None)
                j = nxt('sq', 2)
                p.act(sq[j][:, 0:T], hc[i][:, 0:T], AF.Square, R=['hc%d' % i], W=['sq%d' % j])
                p.mm(ps_stat[:, 0:T], ones_b[:], sq[j][:, 0:T], start=(m == 0), stop=(m == DC - 1),
                     R=['ones_b', 'sq%d' % j], W=['ps_stat'])

        def finish_rstd(segs):
            p.ts(rstd[:, 0:T], ps_stat[:, 0:T], 1.0 / D, ALU.mult, R=['ps_stat'], W=['rstd'], s2=EPS, op1=ALU.add)
            p.act(rstd[:, 0:T], rstd[:, 0:T], AF.Sqrt, R=['rstd'], W=['rstd'])
            p.I('dve', lambda t: t.reciprocal(out=rstd[:, 0:T], in_=rstd[:, 0:T]), R=['rstd'], W=['rstd'])
            if len(segs) > 1:
                p.ts(rstd[:, T:TH], ps_stat_h[:, 0:2], 1.0 / D, ALU.mult, R=['ps_stat_h'], W=['rstd_h'], s2=EPS, op1=ALU.add)
                p.act(rstd[:, T:TH], rstd[:, T:TH], AF.Sqrt, R=['rstd_h'], W=['rstd_h'])
                p.I('dve', lambda t: t.reciprocal(out=rstd[:, T:TH], in_=rstd[:, T:TH]), R=['rstd_h'], W=['rstd_h'])

        def norm_pass2(src, t0, gain, segs, halo_src, to_out=None):
            for m in range(DC):
                i = load_h(src, m, t0, segs, halo_src)
                if to_out is None:
                    p.stt(xb[:, m, :], hc[i][:, 0:T], gain[:, m:m + 1], rstd[:, 0:T], ALU.mult, ALU.mult,
                          R=['hc%d' % i, 'rstd'], W=['xb'])
                    if len(segs) > 1:
                        p.stt(xbh[:, m, :], hc[i][:, T:TH], gain[:, m:m + 1], rstd[:, T:TH], ALU.mult, ALU.mult,
                              R=['hc%d_h' % i, 'rstd_h'], W=['xb_h'])
                else:
                    o = nxt('ho', NHC)
                    p.stt(ho[o][:, 0:T], hc[i][:, 0:T], gain[:, m:m + 1], rstd[:, 0:T], ALU.mult, ALU.mult,
                          R=['hc%d' % i, 'rstd'], W=['ho%d' % o])
                    outkeys.append('out%d' % len(outkeys))
                    p.dma('sp', to_out[m * 128:(m + 1) * 128, t0:t0 + T], ho[o][:, 0:T], R=['ho%d' % o], W=[outkeys[-1]])

        for ti in range(ntiles):
            t0 = ti * T
            segs = [(0, T)]
            hs = len(segs) > 1
            if front:
                for k0 in range(0, FOC, 8):
                    k1 = min(FOC, k0 + 8)
                    p.dma('pool', xb[:, k0:k1, :],
                          yT[k0 * 128:k1 * 128, t0:t0 + T].rearrange("(kc p) t -> p kc t", p=128), W=['xb'])
                if hs:
                    p.dma('pool', xbh[:, 0:FOC, :], yT_halo.rearrange("(kc p) t -> p kc t", p=128), W=['xb_h'])
                p.dma('pool', pb[:], pT[:, t0:t0 + T].rearrange("(kc p) t -> p kc t", p=128), W=['pb'])
                xkeys = ['xb', 'xb_h']
                if GLU:
                    for g in range(1024 // G):
                        wsl, wk = ws.get()
                        for mc in range(G // 128):
                            m = g * (G // 128) + mc
                            outs = mm_group(wsl, wk, mc, 8, lambda kc, si: (xbh if si else xb)[:, 16 + kc, :], xkeys, segs)
                            for si, ((pt, key), (c0, n)) in enumerate(zip(outs, segs)):
                                j = nxt('tg', 2)
                                p.act(gt[j][:, 0:n], pt, AF.Sigmoid, R=[key], W=['gt%d' % j], bias=glub_s[:, m:m + 1])
                                p.tt((ybth if si else ybt)[:, m, :], gt[j][:, 0:n], (xbh if si else xb)[:, 16 + m, :], ALU.mult,
                                     R=['gt%d' % j] + xkeys, W=['ybt'])
                    rhs_fn = lambda kc, si: ((xbh if si else xb)[:, kc, :] if kc < 16 else (ybth if si else ybt)[:, kc - 16, :])
                    rkeys = xkeys + ['ybt']
                else:
                    rhs_fn = lambda kc, si: (xbh if si else xb)[:, kc, :]
                    rkeys = xkeys
                for g in range(D // G):
                    wsl, wk = ws.get()
                    for mc in range(G // 128):
                        m = g * (G // 128) + mc
                        i = load_h(hT_in, m, t0, segs, None)
                        outs = mm_group(wsl, wk, mc, FOC, rhs_fn, rkeys, segs)
                        o = nxt('ho', NHC)
                        p.tt(ho[o][:, 0:T], outs[0][0], hc[i][:, 0:T], ALU.add, R=[outs[0][1], 'hc%d' % i], W=['ho%d' % o])
                        if hs:
                            p.tt(ho[o][:, T:TH], outs[1][0], hc[i][:, T:TH], ALU.add, R=[outs[1][1], 'hc%d_h' % i],
                                 W=['ho%d_h' % o])
                            p.copy(h1halo[:, m, :], ho[o][:, T:TH], R=['ho%d_h' % o], W=['h1halo'], e='act')
                        emit_chunk(m, o, t0, segs, hT_out, True)
                finish_rstd(segs)
                if cfg.get('STOP') == 1:
                    continue
                norm_pass2(hT_out, t0, ffng_s, segs, None)
                for s in range(NS):
                    for g in range(FCS // 2):
                        wg, wgk = ws.get()
                        wu, wuk = ws.get()
                        for mc in range(2):
                            fl = 2 * g + mc
                            fc = s * FCS + fl
                            og = mm_group(wg, wgk, mc, DC, lambda kc, si: (xbh if si else xb)[:, kc, :], ['xb', 'xb_h'], segs)
                            ou = mm_group(wu, wuk, mc, DC, lambda kc, si: (xbh if si else xb)[:, kc, :], ['xb', 'xb_h'], segs)
                            res = []
                            for (outs, ch, cbuf, cname) in ((og, fc, cg, 'cg'), (ou, fc + FC2, cu, 'cu')):
                                e = nxt('E', 4)
                                ek = 'E%d' % e
                                p.act(E[e][:, 2:TH], outs[0][0], AF.Copy, R=[outs[0][1]], W=[ek])
                                if hs:
                                    p.act(E[e][:, 0:2], outs[1][0], AF.Copy, R=[outs[1][1]], W=[ek + 'h'])
                                else:
                                    p.copy(E[e][:, 0:2], halo[:, ch, :], R=['halo%d' % ch], W=[ek + 'h'], e='act')
                                cb_i = (e // 2) % 2
                                ck = '%s%d' % (cname, cb_i)
                                ct = cbuf[cb_i]
                                p.ts(ct[:], E[e][:, 2:TH], cw_s[:, 2, ch:ch + 1], ALU.mult, R=[ek, 'cw', 'cb'], W=[ck],
                                     s2=cb_s[:, ch:ch + 1], op1=ALU.add)
                                p.stt(ct[:], E[e][:, 1:T + 1], cw_s[:, 1, ch:ch + 1], ct[:], ALU.mult, ALU.add,
                                      R=[ek, ek + 'h', 'cw', ck], W=[ck])
                                p.stt(ct[:], E[e][:, 0:T], cw_s[:, 0, ch:ch + 1], ct[:], ALU.mult, ALU.add,
                                      R=[ek, ek + 'h', 'cw', ck], W=[ck])
                                p.copy(halo[:, ch, :], E[e][:, T:TH], R=[ek], W=['halo%d' % ch], e='act')
                                res.append((ct, ck))
                            j = cnt['E'] // 2 % 2
                            p.act(sg[j][:], res[0][0][:], AF.Silu, R=[res[0][1]], W=['sg%d' % j])
                            p.tt(aT[:, fl, :], sg[j][:], res[1][0][:], ALU.mult, R=['sg%d' % j, res[1][1]], W=['aT'])
                    last = (s == NS - 1)
                    for g in range(D // G):
                        wsl, wk = ws.get()
                        for mc in range(G // 128):
                            m = g * (G // 128) + mc
                            i = load_h(hT_out, m, t0, [(0, T)], None)
                            outs = mm_group(wsl, wk, mc, FCS, lambda kc, si: aT[:, kc, :], ['aT'], [(0, T)])
                            o = nxt('ho', NHC)
                            p.tt(ho[o][:, 0:T], outs[0][0], hc[i][:, 0:T], ALU.add, R=[outs[0][1], 'hc%d' % i], W=['ho%d' % o])
                            emit_chunk(m, o, t0, [(0, T)], hT_out, last)
                finish_rstd([(0, T)])
                if cfg.get('STOP') == 2:
                    continue
                norm_pass2(hT_out, t0, pleg_s, [(0, T)], None)
                for g in range(D // G):
                    wsl, wk = ws.get()
                    wi = g % 2
                    p.dma('pool', wpb[wi][:], w_proj[:, g * G:(g + 1) * G].rearrange("(kc p) m -> p kc m", p=128),
                          W=['wpb%d' % wi])
                    for mc in range(G // 128):
                        m = g * (G // 128) + mc
                        i = load_h(hT_out, m, t0, [(0, T)], None)
                        og = mm_group(wsl, wk, mc, DC, lambda kc, si: xb[:, kc, :], ['xb'], [(0, T)])
                        op_ = mm_group(wpb[wi], 'wpb%d' % wi, mc, PC, lambda kc, si: pb[:, kc, :], ['pb'], [(0, T)])
                        j = nxt('tg', 2)
                        p.act(gt[j][:], og[0][0], AF.Sigmoid, R=[og[0][1]], W=['gt%d' % j])
                        p.tt(gt[j][:], gt[j][:], op_[0][0], ALU.mult, R=['gt%d' % j, op_[0][1]], W=['gt%d' % j])
                        o = nxt('ho', NHC)
                        p.tt(ho[o][:, 0:T], gt[j][:], hc[i][:, 0:T], ALU.add, R=['gt%d' % j, 'hc%d' % i], W=['ho%d' % o])
                        emit_chunk(m, o, t0, [(0, T)], hT_out, True)
                finish_rstd([(0, T)])
                hsrc = hT_out
            else:
                stats_from(hT_in, t0)
                finish_rstd([(0, T)])
                hsrc = hT_in
            if FIN:
                norm_pass2(hsrc, t0, tailg_s, [(0, T)], None)
                for g in range(FIN // G):
                    wsl, wk = ws.get()
                    for mc in range(G // 128):
                        m = g * (G // 128) + mc
                        outs = mm_group(wsl, wk, mc, DC, lambda kc, si: xb[:, kc, :], ['xb'], [(0, T)])
                        o = nxt('ho', NHC)
                        p.act(ho[o][:, 0:T], outs[0][0], AF.Copy, R=[outs[0][1]], W=['ho%d' % o])
                        outkeys.append('out%d' % len(outkeys))
                        p.dma('sp', projT[m * 128:(m + 1) * 128, t0:t0 + T], ho[o][:, 0:T], R=['ho%d' % o], W=[outkeys[-1]])
            else:
                norm_pass2(hsrc, t0, tailg_s, [(0, T)], None, to_out=outT)
    return outkeys


GELU_C = 1.5957691216057308


def _gelu_inplace(p, x, xk, tmp, tk):
    p.tt(tmp, x, x, ALU.mult, R=[xk], W=[tk])
    p.ts(tmp, tmp, 0.044715, ALU.mult, R=[tk], W=[tk], s2=1.0, op1=ALU.add)
    p.tt(tmp, tmp, x, ALU.mult, R=[tk, xk], W=[tk])
    p.act(tmp, tmp, AF.Sigmoid, R=[tk], W=[tk], scale=GELU_C)
    p.tt(tmp, tmp, x, ALU.mult, R=[tk, xk], W=[tk])


def sin_reduced(p, dst, dk, ang, ak, off, tmp, tk, ki, kik, kf, kfk):
    PI = math.pi
    C1 = 6.28125
    C2 = 2 * PI - C1
    p.ts(tmp, ang, off, ALU.add, R=[ak], W=[tk])
    p.ts(kf, tmp, 1.0 / (2 * PI), ALU.mult, R=[tk], W=[kfk])
    p.copy(ki, kf, R=[kfk], W=[kik])
    p.copy(kf, ki, R=[kik], W=[kfk])
    p.stt(tmp, kf, -C1, tmp, ALU.mult, ALU.add, R=[kfk, tk], W=[tk])
    p.stt(tmp, kf, -C2, tmp, ALU.mult, ALU.add, R=[kfk, tk], W=[tk])
    p.ts(kf, tmp, PI, ALU.is_gt, R=[tk], W=[kfk], s2=2 * PI, op1=ALU.mult)
    p.tt(tmp, tmp, kf, ALU.subtract, R=[tk, kfk], W=[tk])
    p.ts(kf, tmp, -PI, ALU.is_lt, R=[tk], W=[kfk], s2=2 * PI, op1=ALU.mult)
    p.tt(tmp, tmp, kf, ALU.add, R=[tk, kfk], W=[tk])
    p.act(dst, tmp, AF.Sin, R=[tk], W=[dk], scale=1.0 - 1e-6)


def emit_lru(p, A, S, NB):
    xr, xg, yd, cw, cb, wa, wx, ba, bx, lam = (A[k] for k in ('xr', 'xg', 'yd', 'cw', 'cb', 'wa', 'wx', 'ba', 'bx', 'lam'))
    B = [p.sb([128, S], F32, name="B%d" % i) for i in range(8)]
    xcb = p.sb([128, S], BF16, name="xcb")
    wab = p.sb([128, 128], BF16, name="wab"); wxb = p.sb([128, 128], BF16, name="wxb")
    sm = {k: p.sb([128, NB] if k != 'cw' else [128, NB, 4], F32, name="sm_" + k) for k in ('cw', 'cb', 'ba', 'bx', 'lam', 'c8', 'c16')}
    ps = [p.ps([128, 512], F32, name="ps%d" % i) for i in range(4)]
    for k, src in (('cw', cw), ('cb', cb), ('ba', ba), ('bx', bx), ('lam', lam)):
        p.dma('sp', sm[k][:], src, W=[k])
    p.act(sm['c8'][:], sm['lam'][:], AF.Exp, R=['lam'], W=['c8'], scale=-1.0)
    p.ts(sm['c8'][:], sm['c8'][:], 1.0, ALU.add, R=['c8'], W=['c8'])
    p.act(sm['c8'][:], sm['c8'][:], AF.Ln, R=['c8'], W=['c8'])
    p.ts(sm['c16'][:], sm['c8'][:], -16.0, ALU.mult, R=['c8'], W=['c16'])
    p.ts(sm['c8'][:], sm['c8'][:], -8.0, ALU.mult, R=['c8', 'c16'], W=['c8'])
    NBK = S // 512
    for k in range(NB):
        p.dma('sp', B[0][:], xr[k], W=['B0'])
        p.dma('sp', B[6][:], xg[k], W=['B6'])
        p.dma('pool', wab[:], wa[k], W=['wab'])
        p.dma('pool', wxb[:], wx[k], W=['wxb'])
        w = lambda t: sm['cw'][:, k, t:t + 1]
        p.ts(B[1][:], B[0][:], w(3), ALU.mult, R=['B0', 'cw', 'cb'], W=['B1'], s2=sm['cb'][:, k:k + 1], op1=ALU.add)
        for sh in (1, 2, 3):
            p.stt(B[1][:, sh:], B[0][:, :S - sh], w(3 - sh), B[1][:, sh:], ALU.mult, ALU.add, R=['B0', 'B1', 'cw'], W=['B1'])
        p.act(xcb[:], B[1][:], AF.Copy, R=['B1'], W=['xcb'])
        for nb in range(NBK):
            cs = slice(nb * 512, (nb + 1) * 512)
            i0 = (2 * nb) % 4
            p.mm(ps[i0][:], wab[:], xcb[:, cs], True, True, R=['wab', 'xcb'], W=['ps%d' % i0])
            p.mm(ps[i0 + 1][:], wxb[:], xcb[:, cs], True, True, R=['wxb', 'xcb'], W=['ps%d' % (i0 + 1)])
            p.act(B[0][:, cs], ps[i0][:], AF.Sigmoid, R=['ps%d' % i0, 'ba'], W=['B0'], bias=sm['ba'][:, k:k + 1])
            p.act(B[2][:, cs], ps[i0 + 1][:], AF.Sigmoid, R=['ps%d' % (i0 + 1), 'bx'], W=['B2'], bias=sm['bx'][:, k:k + 1])
        p.act(B[3][:], B[0][:], AF.Exp, R=['B0', 'c8'], W=['B3'], scale=sm['c8'][:, k:k + 1])
        p.act(B[4][:], B[0][:], AF.Exp, R=['B0', 'c16'], W=['B4'], scale=sm['c16'][:, k:k + 1])
        p.ts(B[4][:], B[4][:], -1.0, ALU.mult, R=['B4'], W=['B4'], s2=1.0, op1=ALU.add)
        p.ts(B[4][:], B[4][:], 1e-12, ALU.max, R=['B4'], W=['B4'])
        p.act(B[4][:], B[4][:], AF.Sqrt, R=['B4'], W=['B4'])
        p.tt(B[2][:], B[2][:], B[1][:], ALU.mult, R=['B2', 'B1'], W=['B2'])
        p.tt(B[2][:], B[2][:], B[4][:], ALU.mult, R=['B2', 'B4'], W=['B2'])
        p.I('dve', lambda t: t.tensor_tensor_scan(out=B[5][:], data0=B[3][:], data1=B[2][:], initial=0.0,
                                                  op0=ALU.mult, op1=ALU.add), R=['B3', 'B2'], W=['B5'])
        _gelu_inplace(p, B[6][:], 'B6', B[7][:], 'B7')
        p.tt(B[5][:], B[5][:], B[7][:], ALU.mult, R=['B5', 'B7'], W=['B5'])
        p.dma('sp', yd[k], B[5][:], R=['B5'], W=['ydram_lru%d' % k])


def emit_swa(p, A, S):
    qT, kT, vT, ya, pos, invf, sgn, sinks, maskA, ident = (A[k] for k in ('qT', 'kT', 'vT', 'ya', 'pos', 'invf', 'sgn', 'sinks', 'maskA', 'ident'))
    NBLK = S // 128
    CH = min(S, 1024)
    scale = 128 ** -0.5
    PI = math.pi
    cosT = p.sb([32, S], F32, name="cosT"); sinS = p.sb([32, S], F32, name="sinS")
    posi = p.sb([32, CH], I32, name="posi"); ang = p.sb([32, CH], F32, name="ang"); tmp = p.sb([32, CH], F32, name="tmp")
    ki = p.sb([32, CH], I32, name="ki"); kf_ = p.sb([32, CH], F32, name="kf_")
    xf = p.sb([128, S], F32, name="xf"); sw = p.sb([32, S], F32, name="sw"); rt = p.sb([32, S], F32, name="rt")
    yo = p.sb([128, S], F32, name="yo")
    kb = p.sb([128, S], BF16, name="kb"); vb = p.sb([128, S], BF16, name="vb"); qb = p.sb([128, S], BF16, name="qb")
    V = p.sb([128, NBLK, 128], BF16, name="V")
    sm = {k: p.sb(sh, F32, name="sm_" + k) for k, sh in (('invf', [32, 1]), ('sgn', [32, 1]), ('sinks', [128, 16]),
                                                          ('mask', [128, 256]), ('identf', [128, 128]))}
    identb = p.sb([128, 128], BF16, name="identb")
    NR = 3
    Sm = [p.sb([128, 256], F32, name="Sm%d" % i) for i in range(NR)]
    Wt = [p.sb([128, 256], F32, name="Wt%d" % i) for i in range(NR)]
    Pb = [p.sb([128, 256], BF16, name="Pb%d" % i) for i in range(NR)]
    PT = [p.sb([128, 2, 128], BF16, name="PT%d" % i) for i in range(NR)]
    st = [p.sb([128, 8], F32, name="st%d" % i) for i in range(NR)]
    psS = [p.ps([128, 512], F32, name="psS%d" % i) for i in range(2)]
    psT = [p.ps([128, 1024], BF16, name="psT%d" % i) for i in range(2)]
    psO = [p.ps([128, 512], F32, name="psO%d" % i) for i in range(2)]
    for k, src in (('invf', invf), ('sgn', sgn), ('sinks', sinks), ('mask', maskA), ('identf', ident)):
        p.dma('sp', sm[k][:], src, W=[k])
    p.copy(identb[:], sm['identf'][:], R=['identf'], W=['identb'])
    for c in range(S // CH):
        cs = slice(c * CH, (c + 1) * CH)
        p.dma('sp', posi[:], pos[:, cs], W=['posi'])
        p.copy(ang[:], posi[:], R=['posi'], W=['ang'])
        p.ts(ang[:], ang[:], sm['invf'][:, 0:1], ALU.mult, R=['ang', 'invf'], W=['ang'])
        sin_reduced(p, sinS[:, cs], 'sinS', ang[:], 'ang', 0.0, tmp[:], 'tmp', ki[:], 'ki', kf_[:], 'kf_')
        sin_reduced(p, cosT[:, cs], 'cosT', ang[:], 'ang', 0.5 * PI, tmp[:], 'tmp', ki[:], 'ki', kf_[:], 'kf_')
    p.ts(sinS[:], sinS[:], sm['sgn'][:, 0:1], ALU.mult, R=['sinS', 'sgn'], W=['sinS'])

    def rope_cast(src, xbf, xbk):
        p.dma('sp', xf[:], src, W=['xf'])
        p.dma('sp', sw[0:16, :], src[16:32, :], W=['sw'])
        p.dma('sp', sw[16:32, :], src[0:16, :], W=['sw'])
        p.copy(xbf[:], xf[:], R=['xf'], W=[xbk], e='act')
        p.tt(sw[:], sw[:], sinS[:], ALU.mult, R=['sw', 'sinS'], W=['sw'])
        p.tt(rt[:], xf[0:32, :], cosT[:], ALU.mult, R=['xf', 'cosT'], W=['rt'])
        p.tt(xbf[0:32, :], rt[:], sw[:], ALU.add, R=['rt', 'sw'], W=[xbk])

    it = 0
    for j in range(4):
        rope_cast(kT[j], kb, 'kb')
        p.dma('sp', xf[:], vT[j], W=['xf'])
        p.copy(vb[:], xf[:], R=['xf'], W=['vb'], e='act')
        for i in range(NBLK):
            t = psT[i % 2]
            p.I('pe', lambda e: e.transpose(t[:, 0:128], vb[:, i * 128:(i + 1) * 128], identb[:]),
                R=['vb', 'identb'], W=['psT%d' % (i % 2)])
            p.copy(V[:, i, :], t[:, 0:128], R=['psT%d' % (i % 2)], W=['V'], e='act')
        for hh in range(4):
            h = 4 * j + hh
            rope_cast(qT[h], qb, 'qb')
            for i in range(NBLK):
                r = it % NR
                it += 1
                nk = 256 if i > 0 else 128
                k0 = (i - 1) * 128 if i > 0 else 0
                pS = psS[it % 2]; pSk = 'psS%d' % (it % 2)
                p.mm(pS[:, 0:nk], qb[:, i * 128:(i + 1) * 128], kb[:, k0:k0 + nk], True, True, R=['qb', 'kb'], W=[pSk])
                p.stt(Sm[r][:, 0:nk], pS[:, 0:nk], scale, sm['mask'][:, 256 - nk:256], ALU.mult, ALU.add,
                      R=[pSk, 'mask'], W=['Sm%d' % r])
                sk = 'st%d' % r
                p.I('dve', lambda e: e.reduce_max(out=st[r][:, 0:1], in_=Sm[r][:, 0:nk], axis=AX.X), R=['Sm%d' % r], W=[sk + 'a'])
                p.ts(st[r][:, 1:2], st[r][:, 0:1], sm['sinks'][:, h:h + 1], ALU.max, R=[sk + 'a', 'sinks'], W=[sk + 'b'],
                     s2=-1.0, op1=ALU.mult)
                p.act(Wt[r][:, 0:nk], Sm[r][:, 0:nk], AF.Exp, R=['Sm%d' % r, sk + 'b'], W=['Wt%d' % r, sk + 'c'],
                      bias=st[r][:, 1:2], accum_out=st[r][:, 2:3])
                p.act(st[r][:, 3:4], sm['sinks'][:, h:h + 1], AF.Exp, R=['sinks', sk + 'b'], W=[sk + 'd'], bias=st[r][:, 1:2])
                p.tt(st[r][:, 4:5], st[r][:, 2:3], st[r][:, 3:4], ALU.add, R=[sk + 'c', sk + 'd'], W=[sk + 'e'])
                p.I('dve', lambda e: e.reciprocal(out=st[r][:, 5:6], in_=st[r][:, 4:5]), R=[sk + 'e'], W=[sk + 'f'])
                p.ts(Pb[r][:, 0:nk], Wt[r][:, 0:nk], st[r][:, 5:6], ALU.mult, R=['Wt%d' % r, sk + 'f'], W=['Pb%d' % r])
                pT_ = psT[it % 2]; pTk = 'psT%d' % (it % 2)
                nkb = nk // 128
                for jj in range(nkb):
                    p.I('pe', lambda e: e.transpose(pT_[:, jj * 128:(jj + 1) * 128], Pb[r][:, jj * 128:(jj + 1) * 128], identb[:]),
                        R=['Pb%d' % r, 'identb'], W=[pTk])
                p.copy(PT[r][:, 0:nkb, :], pT_[:, 0:nk].rearrange("p (j q) -> p j q", j=nkb), R=[pTk], W=['PT%d' % r], e='act')
                pO = psO[it % 2]; pOk = 'psO%d' % (it % 2)
                for jj in range(nkb):
                    blk = (k0 // 128) + jj
                    p.mm(pO[:, 0:128], V[:, blk, :], PT[r][:, jj, :], jj == 0, jj == nkb - 1, R=['V', 'PT%d' % r], W=[pOk])
                p.copy(yo[:, i * 128:(i + 1) * 128], pO[:, 0:128], R=[pOk], W=['yo'], e='act')
            p.dma('sp', ya[h], yo[:], R=['yo'], W=['ydram_swa%d' % h])


def emit_sb(p, A, S, NH):
    qT, kT, vT, yc, negtri, ones, mask, ident = (A[k] for k in ('qT', 'kT', 'vT', 'yc', 'negtri', 'ones', 'mask', 'ident'))
    NBLK = S // 128
    NQ = S // 512
    scale = 128 ** -0.5
    xf = [p.sb([128, S], F32, name="xf%d" % i) for i in range(2)]
    yo = p.sb([128, S], F32, name="yo")
    qb = p.sb([128, S], BF16, name="qb"); kb = p.sb([128, S], BF16, name="kb"); vb = p.sb([128, S], BF16, name="vb")
    V = p.sb([128, NBLK, 128], BF16, name="V")
    cf = p.sb([128, 128], F32, name="cf")
    negtrib = p.sb([128, 128], BF16, name="negtrib"); onesb = p.sb([128, 128], BF16, name="onesb"); identb = p.sb([128, 128], BF16, name="identb")
    maskf = p.sb([128, 4, 512], F32, name="maskf")
    maskb = p.sb([128, 4, 512], BF16, name="maskb")
    spb = [p.sb([128, 512], BF16, name="spb%d" % i) for i in range(2)]
    Xs = [p.sb([128, 512], F32, name="Xs%d" % i) for i in range(2)]
    Ab = [p.sb([128, 512], BF16, name="Ab%d" % i) for i in range(2)]
    Rs = p.sb([128, 512], F32, name="Rs")
    Et = [p.sb([128, 512], F32, name="Et%d" % i) for i in range(2)]
    psA = [p.ps([128, 512], F32, name="psA%d" % i) for i in range(2)]
    psB = [p.ps([128, 512], F32, name="psB%d" % i) for i in range(2)]
    psC = p.ps([128, 512], F32, name="psC")
    psO = [p.ps([128, 512], F32, name="psO%d" % i) for i in range(2)]
    psT = p.ps([128, 1024], BF16, name="psT")
    for (src, dstb, nm) in ((negtri, negtrib, 'negtrib'), (ones, onesb, 'onesb'), (ident, identb, 'identb')):
        p.dma('sp', cf[:], src, W=['cf'])
        p.copy(dstb[:], cf[:], R=['cf'], W=[nm])
    p.dma('sp', maskf[:], mask, W=['maskf'])
    p.copy(maskb[:], maskf[:], R=['maskf'], W=['maskb'])
    it = 0
    for h in range(NH):
        p.dma('sp', xf[0][:], qT[h], W=['xf0'])
        p.copy(qb[:], xf[0][:], R=['xf0'], W=['qb'], e='act')
        p.dma('sp', xf[1][:], kT[h], W=['xf1'])
        p.act(kb[:], xf[1][:], AF.Copy, R=['xf1'], W=['kb'], scale=scale)
        p.dma('sp', xf[0][:], vT[h], W=['xf0'])
        p.copy(vb[:], xf[0][:], R=['xf0'], W=['vb'], e='act')
        for i in range(NBLK):
            c = (i % 8) * 128
            p.I('pe', lambda e: e.transpose(psT[:, c:c + 128], vb[:, i * 128:(i + 1) * 128], identb[:]),
                R=['vb', 'identb'], W=['psT'])
            p.copy(V[:, i, :], psT[:, c:c + 128], R=['psT'], W=['V'], e='act')
        for Q in range(NQ):
            qs = slice(Q * 512, (Q + 1) * 512)
            nkb = 4 * Q + 4
            pO = psO[Q % 2]; pOk = 'psO%d' % (Q % 2)
            for idx, kbi in enumerate(range(nkb - 1, -1, -1)):
                b = it % 2
                it += 1
                r = kbi - 4 * Q
                ks = slice(kbi * 128, (kbi + 1) * 128)
                first = (idx == 0)
                p.mm(psA[b][:], kb[:, ks], qb[:, qs], True, True, R=['kb', 'qb'], W=['psA%d' % b])
                p.act(Et[b][:], psA[b][:], AF.Exp, R=['psA%d' % b], W=['Et%d' % b])
                p.act(spb[b][:], Et[b][:], AF.Ln, R=['Et%d' % b], W=['spb%d' % b], bias=1.0)
                if r >= 0:
                    p.tt(spb[b][:], spb[b][:], maskb[:, r, :], ALU.mult, R=['spb%d' % b, 'maskb'], W=['spb%d' % b])
                p.mm(psB[b][:], kb[:, ks], qb[:, qs], True, False, R=['kb', 'qb'], W=['psB%d' % b])
                p.mm(psB[b][:], negtrib[:], spb[b][:], False, True, R=['negtrib', 'spb%d' % b], W=['psB%d' % b])
                if first:
                    p.act(Ab[b][:], psB[b][:], AF.Exp, R=['psB%d' % b], W=['Ab%d' % b])
                else:
                    p.tt(Xs[b][:], psB[b][:], Rs[:], ALU.subtract, R=['psB%d' % b, 'Rs'], W=['Xs%d' % b])
                    p.act(Ab[b][:], Xs[b][:], AF.Exp, R=['Xs%d' % b], W=['Ab%d' % b])
                if r >= 0:
                    p.tt(Ab[b][:], Ab[b][:], maskb[:, r, :], ALU.mult, R=['Ab%d' % b, 'maskb'], W=['Ab%d' % b])
                if kbi > 0:
                    p.mm(psC[:], onesb[:], spb[b][:], True, True, R=['onesb', 'spb%d' % b], W=['psC'])
                    if first:
                        p.copy(Rs[:], psC[:], R=['psC'], W=['Rs'])
                    else:
                        p.tt(Rs[:], Rs[:], psC[:], ALU.add, R=['Rs', 'psC'], W=['Rs'])
                p.mm(pO[:], V[:, kbi, :], Ab[b][:], first, kbi == 0, R=['V', 'Ab%d' % b], W=[pOk])
            p.copy(yo[:, qs], pO[:], R=[pOk], W=['yo'], e='act')
        p.dma('sp', yc[h], yo[:], R=['yo'], W=['ydram_sb%d' % h])


def sb_consts():
    j = np.arange(128)[:, None]; s = np.arange(128)[None, :]
    negtri = np.where(j >= s, -1.0, 0.0).astype(np.float32)
    sk = np.arange(128)[:, None, None]; rr = np.arange(4)[None, :, None]; t = np.arange(512)[None, None, :]
    mask = (sk + 128 * rr < t).astype(np.float32)
    return dict(negtri=negtri, ones=np.ones((128, 128), np.float32), mask=_c(mask), ident=np.eye(128, dtype=np.float32))


def emit_s5(p, A, S, NQ4):
    u, gT, prm_p, prm_b, Bre, Bim, Cre, Cim, d_p = (A[k] for k in ('u', 'g', 'prm_p', 'prm_b', 'Bre', 'Bim', 'Cre', 'Cim', 'd_p'))
    NBK = S // 512
    NLV = int(math.log2(S))
    bbre = p.sb([128, 8, 128], BF16, name="bbre"); bbim = p.sb([128, 8, 128], BF16, name="bbim")
    creb = p.sb([128, 8, 128], BF16, name="creb"); ncimb = p.sb([128, 8, 128], BF16, name="ncimb")
    Are = p.sb([128, NLV, 8], F32, name="Are"); Aim = p.sb([128, NLV, 8], F32, name="Aim"); nAim = p.sb([128, NLV, 8], F32, name="nAim")
    dps = p.sb([128, 2], F32, name="dps")
    ps = [p.ps([128, 512], F32, name="ps%d" % i) for i in range(4)]
    outer = p.es

    def disc(prm, n, tag, need_f):
        t = {k: p.sb([128, n], F32, name="%s_%s" % (tag, k)) for k in ('dt', 'lr', 'li', 'x', 'ang', 'cos', 'sin', 'are', 'aim', 'tmp', 'kf')}
        ki = p.sb([128, n], I32, name=tag + "_ki")
        K = lambda k: tag + k
        pin = p.sb([128, 3, n], F32, name=tag + "_pin")
        p.dma('sp', pin[:], prm, W=[K('pin')])
        p.act(t['dt'][:], pin[:, 2, :], AF.Exp, R=[K('pin')], W=[K('dt')])
        p.ts(t['lr'][:], pin[:, 0, :], -1e-4, ALU.min, R=[K('pin')], W=[K('lr')])
        p.copy(t['li'][:], pin[:, 1, :], R=[K('pin')], W=[K('li')])
        p.tt(t['x'][:], t['lr'][:], t['dt'][:], ALU.mult, R=[K('lr'), K('dt')], W=[K('x')])
        p.act(t['x'][:], t['x'][:], AF.Exp, R=[K('x')], W=[K('x')])
        p.tt(t['ang'][:], t['li'][:], t['dt'][:], ALU.mult, R=[K('li'), K('dt')], W=[K('ang')])
        sin_reduced(p, t['sin'][:], K('sin'), t['ang'][:], K('ang'), 0.0, t['tmp'][:], K('tmp'), ki[:], K('ki'), t['kf'][:], K('kf'))
        sin_reduced(p, t['cos'][:], K('cos'), t['ang'][:], K('ang'), 0.5 * math.pi, t['tmp'][:], K('tmp'), ki[:], K('ki'), t['kf'][:], K('kf'))
        p.tt(t['are'][:], t['x'][:], t['cos'][:], ALU.mult, R=[K('x'), K('cos')], W=[K('are')])
        p.tt(t['aim'][:], t['x'][:], t['sin'][:], ALU.mult, R=[K('x'), K('sin')], W=[K('aim')])
        if need_f:
            lr, li, are, aim = t['lr'], t['li'], t['are'], t['aim']
            nr, den, t1, t2, fre, fim = t['cos'], t['sin'], t['tmp'], t['kf'], t['x'], t['ang']
            p.ts(nr[:], are[:], -1.0, ALU.add, R=[K('are')], W=[K('cos')])
            p.tt(den[:], lr[:], lr[:], ALU.mult, R=[K('lr')], W=[K('sin')])
            p.tt(t1[:], li[:], li[:], ALU.mult, R=[K('li')], W=[K('tmp')])
            p.tt(den[:], den[:], t1[:], ALU.add, R=[K('sin'), K('tmp')], W=[K('sin')])
            p.I('dve', lambda e: e.reciprocal(out=den[:], in_=den[:]), R=[K('sin')], W=[K('sin')])
            p.tt(t1[:], nr[:], lr[:], ALU.mult, R=[K('cos'), K('lr')], W=[K('tmp')])
            p.tt(t2[:], aim[:], li[:], ALU.mult, R=[K('aim'), K('li')], W=[K('kf')])
            p.tt(t1[:], t1[:], t2[:], ALU.add, R=[K('tmp'), K('kf')], W=[K('tmp')])
            p.tt(fre[:], t1[:], den[:], ALU.mult, R=[K('tmp'), K('sin')], W=[K('x')])
            p.tt(t1[:], aim[:], lr[:], ALU.mult, R=[K('aim'), K('lr')], W=[K('tmp')])
            p.tt(t2[:], nr[:], li[:], ALU.mult, R=[K('cos'), K('li')], W=[K('kf')])
            p.tt(t1[:], t1[:], t2[:], ALU.subtract, R=[K('tmp'), K('kf')], W=[K('tmp')])
            p.tt(fim[:], t1[:], den[:], ALU.mult, R=[K('tmp'), K('sin')], W=[K('ang')])
            return dict(fre=(fre, K('x')), fim=(fim, K('ang')), t1=(t1, K('tmp')), t2=(t2, K('kf')))
        return dict(are=(t['are'], K('are')), aim=(t['aim'], K('aim')))

    npc = 0
    for q4 in range(NQ4):
        with ExitStack() as sub:
            p.es = sub
            fb = disc(prm_b[q4], 1024, 'b', True)
            Bf = {k: p.sb([128, 8, 128], F32, name="Bf" + k) for k in ('re', 'im')}
            p.dma('sp', Bf['re'][:], Bre[q4], W=['Bfre']); p.dma('sp', Bf['im'][:], Bim[q4], W=['Bfim'])
            fre, frek = fb['fre']; fim, fimk = fb['fim']; t1, t1k = fb['t1']; t2, t2k = fb['t2']
            fl = lambda tile_: tile_[:].rearrange("p s m -> p (s m)")
            p.tt(t1[:], fre[:], fl(Bf['re']), ALU.mult, R=[frek, 'Bfre'], W=[t1k])
            p.tt(t2[:], fim[:], fl(Bf['im']), ALU.mult, R=[fimk, 'Bfim'], W=[t2k])
            p.tt(fl(bbre), t1[:], t2[:], ALU.subtract, R=[t1k, t2k], W=['bbre'])
            p.tt(t1[:], fre[:], fl(Bf['im']), ALU.mult, R=[frek, 'Bfim'], W=[t1k])
            p.tt(t2[:], fim[:], fl(Bf['re']), ALU.mult, R=[fimk, 'Bfre'], W=[t2k])
            p.tt(fl(bbim), t1[:], t2[:], ALU.add, R=[t1k, t2k], W=['bbim'])
            p.dma('sp', Bf['re'][:], Cre[q4], W=['Bfre']); p.dma('sp', Bf['im'][:], Cim[q4], W=['Bfim'])
            p.copy(creb[:], Bf['re'][:], R=['Bfre'], W=['creb'])
            p.ts(ncimb[:], Bf['im'][:], -1.0, ALU.mult, R=['Bfim'], W=['ncimb'])
            fp = disc(prm_p[q4], 8, 'p', False)
            sq1 = p.sb([128, 8], F32, name="sq1"); sq2 = p.sb([128, 8], F32, name="sq2")
            p.copy(Are[:, 0, :], fp['are'][0][:], R=[fp['are'][1]], W=['Are'])
            p.copy(Aim[:, 0, :], fp['aim'][0][:], R=[fp['aim'][1]], W=['Aim'])
            for lv in range(1, NLV):
                p.tt(sq1[:], Are[:, lv - 1, :], Are[:, lv - 1, :], ALU.mult, R=['Are'], W=['sq1'])
                p.tt(sq2[:], Aim[:, lv - 1, :], Aim[:, lv - 1, :], ALU.mult, R=['Aim'], W=['sq2'])
                p.tt(sq2[:], sq1[:], sq2[:], ALU.subtract, R=['sq1', 'sq2'], W=['sq2'])
                p.tt(sq1[:], Are[:, lv - 1, :], Aim[:, lv - 1, :], ALU.mult, R=['Are', 'Aim'], W=['sq1'])
                p.copy(Are[:, lv, :], sq2[:], R=['sq2'], W=['Are'])
                p.ts(Aim[:, lv, :], sq1[:], 2.0, ALU.mult, R=['sq1'], W=['Aim'])
            p.ts(nAim[:], Aim[:], -1.0, ALU.mult, R=['Aim'], W=['nAim'])
            p.dma('sp', dps[:], d_p[q4], W=['dps'])
            p.barrier()
        p.es = outer
        mainscope = ExitStack()
        p.es = mainscope
        uf = p.sb([128, S], F32, name="uf"); ub = p.sb([128, S], BF16, name="ub")
        ysb = p.sb([128, S], F32, name="ysb"); gtmp = p.sb([128, S], F32, name="gtmp")
        X = [[p.sb([128, S], F32, name="X%d%s" % (i, c)) for c in ('r', 'i')] for i in range(2)]
        hb = [p.sb([128, S], BF16, name="hb" + c) for c in ('r', 'i')]
        for cg in range(2):
            p.dma('sp', uf[:], u[q4, cg], W=['uf'])
            p.copy(ub[:], uf[:], R=['uf'], W=['ub'], e='act')
            for s4 in range(4):
                s = cg * 4 + s4
                for (bbt, bk, c) in ((bbre, 'bbre', 0), (bbim, 'bbim', 1)):
                    for nb in range(NBK):
                        cs = slice(nb * 512, (nb + 1) * 512)
                        i = npc % 4; npc += 1
                        p.mm(ps[i][:], bbt[:, s, :], ub[:, cs], True, True, R=[bk, 'ub'], W=['ps%d' % i])
                        p.copy(X[0][c][:, cs], ps[i][:], R=['ps%d' % i], W=['X0' + 'ri'[c]], e='act')
                cur = 0
                for lv in range(NLV):
                    d = 1 << lv
                    A_, Bn = X[cur], X[1 - cur]
                    ak, bk_ = 'X%d' % cur, 'X%d' % (1 - cur)
                    ar, ai, nai = Are[:, lv, s:s + 1], Aim[:, lv, s:s + 1], nAim[:, lv, s:s + 1]
                    p.stt(Bn[0][:, d:], A_[0][:, :S - d], ar, A_[0][:, d:], ALU.mult, ALU.add, R=[ak + 'r', 'Are'], W=[bk_ + 'r'])
                    p.stt(Bn[0][:, d:], A_[1][:, :S - d], nai, Bn[0][:, d:], ALU.mult, ALU.add, R=[ak + 'i', 'nAim', bk_ + 'r'], W=[bk_ + 'r'])
                    p.copy(Bn[0][:, :d], A_[0][:, :d], R=[ak + 'r'], W=[bk_ + 'r'], e='act')
                    p.stt(Bn[1][:, d:], A_[1][:, :S - d], ar, A_[1][:, d:], ALU.mult, ALU.add, R=[ak + 'i', 'Are'], W=[bk_ + 'i'])
                    p.stt(Bn[1][:, d:], A_[0][:, :S - d], ai, Bn[1][:, d:], ALU.mult, ALU.add, R=[ak + 'r', 'Aim', bk_ + 'i'], W=[bk_ + 'i'])
                    p.copy(Bn[1][:, :d], A_[1][:, :d], R=[ak + 'i'], W=[bk_ + 'i'], e='act')
                    cur = 1 - cur
                Hk = 'X%d' % cur
                p.copy(hb[0][:], X[cur][0][:], R=[Hk + 'r'], W=['hbr'], e='act')
                p.copy(hb[1][:], X[cur][1][:], R=[Hk + 'i'], W=['hbi'], e='act')
                rows = slice(32 * s4, 32 * s4 + 32)
                for nb in range(NBK):
                    cs = slice(nb * 512, (nb + 1) * 512)
                    i = npc % 4; npc += 1
                    p.mm(ps[i][:], creb[:, s, :], hb[0][:, cs], True, False, R=['creb', 'hbr'], W=['ps%d' % i])
                    p.mm(ps[i][:], ncimb[:, s, :], hb[1][:, cs], False, True, R=['ncimb', 'hbi'], W=['ps%d' % i])
                    p.stt(ysb[rows, cs], uf[rows, cs], dps[rows, cg:cg + 1], ps[i][rows, :], ALU.mult, ALU.add,
                          R=['uf', 'dps', 'ps%d' % i], W=['ysb'])
            _gelu_inplace(p, ysb[:], 'ysb', gtmp[:], 'gtmp')
            p.dma('sp', gT[q4, cg], gtmp[:], R=['gtmp'], W=['ydram_s5_%d_%d' % (q4, cg)])
        p.barrier()
        mainscope.close()
        p.es = outer


def s5_layout(lam_re, lam_im, log_dt, b_re, b_im, c_re, c_im, d):
    prm_p = np.zeros((4, 128, 3, 8), np.float32); prm_b = np.zeros((4, 128, 3, 1024), np.float32)
    Bre = np.zeros((4, 128, 8, 128), np.float32); Bim = np.zeros((4, 128, 8, 128), np.float32)
    Cre = np.zeros((4, 128, 8, 128), np.float32); Cim = np.zeros((4, 128, 8, 128), np.float32)
    d_p = np.zeros((4, 128, 2), np.float32)
    for j in range(4):
        gs = slice(16 * j, 16 * j + 16)
        prm = np.stack([lam_re[gs].reshape(1024), lam_im[gs].reshape(1024), np.repeat(log_dt[gs], 64)]).astype(np.float32)
        prm_p[j] = prm.reshape(3, 8, 128).transpose(2, 0, 1)
        prm_b[j] = np.broadcast_to(prm[None], (128, 3, 1024))
        for s in range(8):
            for gg in range(2):
                g = 16 * j + 2 * s + gg
                c0 = 32 * (s % 4) + 16 * gg
                m0 = 64 * gg
                Bre[j, c0:c0 + 16, s, m0:m0 + 64] = b_re[g].T
                Bim[j, c0:c0 + 16, s, m0:m0 + 64] = b_im[g].T
                Cre[j, m0:m0 + 64, s, c0:c0 + 16] = c_re[g].T
                Cim[j, m0:m0 + 64, s, c0:c0 + 16] = c_im[g].T
        d_p[j] = d[256 * j:256 * j + 256].reshape(2, 128).T
    return dict(prm_p=prm_p, prm_b=prm_b, Bre=Bre, Bim=Bim, Cre=Cre, Cim=Cim, d_p=d_p)


def _c(a):
    return np.ascontiguousarray(a)


def _col(v):
    return _c(np.asarray(v, np.float32).reshape(-1, 128).T)


def build_fused(cfg):
    D, DFF, S, DEPTH, T, NS = (cfg[k] for k in ('D', 'DFF', 'S', 'DEPTH', 'T', 'NS'))
    NE, NO = (DEPTH + 1) // 2, DEPTH // 2
    DC, FC2 = D // 128, DFF // 128
    nc = bass.Bass("TRN2", target_bir_lowering=False)
    din = lambda name, shape, dt=F32: nc.dram_tensor(name, list(shape), dt, kind="ExternalInput").ap()
    scr = lambda name, shape: nc.dram_tensor(name, list(shape), F32, kind="Internal").ap()
    I = {}
    for name, shape in (('xT', [D, S]), ('pT', [DEPTH, 256, S]), ('ones', [128, 128]), ('ident', [128, 128]), ('maskA', [128, 256]),
                        ('invf', [32, 1]), ('sgn', [32, 1]), ('negtri', [128, 128]), ('sbmask', [128, 4, 512]),
                        ('mixg', [DEPTH, 128, DC]), ('ffng', [DEPTH, 128, DC]), ('pleg', [DEPTH, 128, DC]), ('fing', [128, DC]),
                        ('ab_w_in', [NE, 4096 // 256, 128, DC * 256]), ('ab_w_out', [NE, D // 256, 128, 24 * 256]), ('sinks', [NE, 128, 16]),
                        ('prm_p', [NE, 4, 128, 3, 8]), ('prm_b', [NE, 4, 128, 3, 1024]),
                        ('Bre', [NE, 4, 128, 8, 128]), ('Bim', [NE, 4, 128, 8, 128]), ('Cre', [NE, 4, 128, 8, 128]), ('Cim', [NE, 4, 128, 8, 128]),
                        ('d_p', [NE, 4, 128, 2]), ('glu_w', [NE, 4, 128, 8 * 256]), ('glu_b', [NE, 128, 8]),
                        ('cd_w_in', [NO, 10240 // 256, 128, DC * 256]), ('cd_w_out', [NO, D // 256, 128, 32 * 256]),
                        ('lcw', [NO, 128, 16, 4]), ('lcb', [NO, 128, 16]), ('lwa', [NO, 16, 128, 128]), ('lwx', [NO, 16, 128, 128]),
                        ('lba', [NO, 128, 16]), ('lbx', [NO, 128, 16]), ('llam', [NO, 128, 16]),
                        ('w_up', [DEPTH, 2 * DFF // 256, 128, DC * 256]), ('conv_w', [DEPTH, 128, 3, 2 * FC2]), ('conv_b', [DEPTH, 128, 2 * FC2]),
                        ('w_down', [DEPTH, NS * (D // 256), 128, (FC2 // NS) * 256]), ('w_gate', [DEPTH, D // 256, 128, DC * 256]), ('w_proj', [DEPTH, 256, D])):
        I[name] = din(name, shape)
    I['pos'] = din('pos', [32, S], I32)
    outT = nc.dram_tensor("outT", [D, S], F32, kind="ExternalOutput").ap()
    hA = scr("hA", [D, S]); proj = scr("proj", [10240, S]); yT = scr("yT", [4096, S])
    hd = lambda ap: ap.rearrange("(h p) s -> h p s", p=128)

    def dcfg(FO, GLU, FIN):
        return dict(D=D, FO=FO, GLU=GLU, DFF=DFF, FIN=FIN, NT=S, T=T, PD=256, NS=NS)

    with ExitStack() as ges:
        p = Prog(nc, ges)

        def phase(fn):
            with ExitStack() as pes:
                p.es = pes
                fn()
                p.barrier()
            p.es = ges

        phase(lambda: emit_dense(p, dcfg(0, False, 4096), dict(hT_in=I['xT'], ones=I['ones'], tail_g=I['mixg'][0],
                                                                w_in=I['ab_w_in'][0], projT=proj[0:4096])))
        for l in range(DEPTH):
            jl = l // 2
            if l % 2 == 0:
                phase(lambda: emit_swa(p, dict(qT=hd(proj[0:2048]), kT=hd(proj[2048:2560]), vT=hd(proj[2560:3072]), ya=hd(yT[0:2048]),
                                               pos=I['pos'], invf=I['invf'], sgn=I['sgn'], sinks=I['sinks'][jl], maskA=I['maskA'],
                                               ident=I['ident']), S))
                q4 = lambda ap: ap.rearrange("(q c p) s -> q c p s", q=4, c=2, p=128)
                phase(lambda: emit_s5(p, dict(u=q4(proj[3072:4096]), g=q4(yT[2048:3072]), prm_p=I['prm_p'][jl], prm_b=I['prm_b'][jl],
                                              Bre=I['Bre'][jl], Bim=I['Bim'][jl], Cre=I['Cre'][jl], Cim=I['Cim'][jl], d_p=I['d_p'][jl]), S, 4))
                FO, GLU, w_out = 3072, True, I['ab_w_out'][jl]
            else:
                phase(lambda: emit_sb(p, dict(qT=hd(proj[0:2048]), kT=hd(proj[2048:4096]), vT=hd(proj[4096:6144]), yc=hd(yT[0:2048]),
                                              negtri=I['negtri'], ones=I['ones'], mask=I['sbmask'], ident=I['ident']), S, 16))
                phase(lambda: emit_lru(p, dict(xr=hd(proj[6144:8192]), xg=hd(proj[8192:10240]), yd=hd(yT[2048:4096]), cw=I['lcw'][jl],
                                               cb=I['lcb'][jl], wa=I['lwa'][jl], wx=I['lwx'][jl], ba=I['lba'][jl], bx=I['lbx'][jl],
                                               lam=I['llam'][jl]), S, 16))
                FO, GLU, w_out = 4096, False, I['cd_w_out'][jl]
            A = dict(hT_in=(I['xT'] if l == 0 else hA), hT_out=hA, ones=I['ones'], yT=yT[0:FO], pT=I['pT'][l], w_out=w_out,
                     ffn_g=I['ffng'][l], w_up=I['w_up'][l], conv_w=I['conv_w'][l], conv_b=I['conv_b'][l], w_down=I['w_down'][l],
                     ple_g=I['pleg'][l], w_gate=I['w_gate'][l], w_proj=I['w_proj'][l])
            if GLU:
                A.update(glu_w=I['glu_w'][jl], glu_b=I['glu_b'][jl])
            if l == DEPTH - 1:
                FIN = 0
                A.update(tail_g=I['fing'], outT=outT)
            elif (l + 1) % 2 == 0:
                FIN = 4096
                A.update(tail_g=I['mixg'][l + 1], w_in=I['ab_w_in'][(l + 1) // 2], projT=proj[0:4096])
            else:
                FIN = 10240
                A.update(tail_g=I['mixg'][l + 1], w_in=I['cd_w_in'][(l + 1) // 2], projT=proj[0:10240])
            phase(lambda: emit_dense(p, dcfg(FO, GLU, FIN), A))
    return nc


def _prelay(W):
    K, M = W.shape
    KC, NG = K // 128, M // 256
    return np.ascontiguousarray(W.reshape(KC, 128, NG, 256).transpose(2, 1, 0, 3)).reshape(NG, 128, KC * 256)


def prelayL(arr, nsplit=1):
    arr = np.asarray(arr, np.float32)
    out = []
    for W in arr:
        R = W.shape[0] // nsplit
        out.append(np.concatenate([_prelay(W[i * R:(i + 1) * R]) for i in range(nsplit)], axis=0))
    return np.stack(out)


_PROGS = {}


def kernel(x, p, positions, mix_norm, ffn_norm, ple_norm, final_norm,
           ab_w_in, ab_w_out, attn_sinks,
           s5_lam_re, s5_lam_im, s5_log_dt, s5_b_re, s5_b_im, s5_c_re, s5_c_im,
           s5_d, s5_glu_w, s5_glu_b,
           cd_w_in, cd_w_out, lru_conv_w, lru_conv_b, lru_wa, lru_ba, lru_wx, lru_bx, lru_lambda,
           ffn_w_up, ffn_conv_w, ffn_conv_b, ffn_w_down, ple_w_gate, ple_w_proj):
    f = lambda a: np.asarray(a, np.float32)
    x = f(x)
    Bz, S, D = x.shape
    DEPTH = np.asarray(p).shape[0]
    DFF = np.asarray(ffn_w_down).shape[1]
    NE, NO = (DEPTH + 1) // 2, DEPTH // 2
    cfg = dict(D=D, DFF=DFF, S=S, DEPTH=DEPTH, T=512, NS=(4 if DFF >= 4096 else 2))
    key = tuple(sorted(cfg.items()))
    if key not in _PROGS:
        _PROGS[key] = build_fused(cfg)
    nc = _PROGS[key]
    inv = (500000.0 ** (-np.arange(16, dtype=np.float32) * (2.0 / 32))).astype(np.float32)
    qi = np.arange(128)[:, None]; kj = np.arange(256)[None, :]
    sbc = sb_consts()
    s5l = [s5_layout(f(s5_lam_re[j]), f(s5_lam_im[j]), f(s5_log_dt[j]), f(s5_b_re[j]), f(s5_b_im[j]), f(s5_c_re[j]), f(s5_c_im[j]),
                     f(s5_d[j])) for j in range(NE)]
    colL = lambda arr: _c(np.stack([_col(v) for v in f(arr)]))
    shared = dict(
        ones=np.ones((128, 128), np.float32), ident=np.eye(128, dtype=np.float32),
        maskA=np.where((kj > qi) & (kj <= qi + 128), 0.0, -30000.0).astype(np.float32),
        invf=np.concatenate([inv, inv]).reshape(32, 1).astype(np.float32),
        sgn=np.concatenate([-np.ones(16, np.float32), np.ones(16, np.float32)]).reshape(32, 1),
        negtri=sbc['negtri'], sbmask=sbc['mask'],
        mixg=colL(mix_norm), ffng=colL(ffn_norm), pleg=colL(ple_norm), fing=_col(final_norm),
        ab_w_in=prelayL(ab_w_in), ab_w_out=prelayL(ab_w_out),
        sinks=_c(np.broadcast_to(f(attn_sinks)[:, None, :], (NE, 128, 16))),
        glu_w=prelayL(s5_glu_w), glu_b=colL(s5_glu_b),
        cd_w_in=prelayL(cd_w_in), cd_w_out=prelayL(cd_w_out),
        lcw=_c(f(lru_conv_w).reshape(NO, 4, 16, 128).transpose(0, 3, 2, 1)), lcb=colL(lru_conv_b),
        lwa=f(lru_wa), lwx=f(lru_wx), lba=colL(f(lru_ba).reshape(NO, -1)), lbx=colL(f(lru_bx).reshape(NO, -1)),
        llam=colL(lru_lambda),
        w_up=prelayL(ffn_w_up), conv_w=_c(f(ffn_conv_w).reshape(DEPTH, 3, -1, 128).transpose(0, 3, 1, 2)), conv_b=colL(ffn_conv_b),
        w_down=prelayL(ffn_w_down, cfg['NS']), w_gate=prelayL(ple_w_gate), w_proj=f(ple_w_proj))
    for k in ('prm_p', 'prm_b', 'Bre', 'Bim', 'Cre', 'Cim', 'd_p'):
        shared[k] = _c(np.stack([s5l[j][k] for j in range(NE)]))
    positions = np.asarray(positions)
    pp = f(p)
    ims = []
    for b in range(Bz):
        d_ = dict(xT=_c(x[b].T), pT=_c(pp[:, b].transpose(0, 2, 1)),
                  pos=_c(np.broadcast_to(positions[b].astype(np.int32)[None, :], (32, S))))
        d_.update(shared)
        ims.append(d_)
    res = run_bass_kernel_spmd(nc, ims, core_ids=list(range(Bz)))
    out = np.stack([res.results[b]["outT"].T for b in range(Bz)])
    return _c(out).astype(np.float32)
```
